# Optimizing a Trainium2 kernel written in Bass

```python
import jax, jax.numpy as jnp
from jax import lax
import numpy as np

D_MODEL = 1024
BATCH = 4
SEQ = 4096
DEPTH = 2

D_MIX = D_MODEL
NSA_HEADS = 8
NSA_KV_GROUPS = 2
NSA_HEAD_DIM = 64
NSA_REP = NSA_HEADS // NSA_KV_GROUPS
CMP_BLOCK = 32
CMP_STRIDE = 16
CMP_HIDDEN = 128
SEL_BLOCK = 64
SEL_TOPN = 16
WINDOW = 512
Q_BLOCK = 128
HG_HEADS = 4
HG_DK = 128
HG_DV = 128
HG_CHUNK = 64
D_FF = 2816
ROPE_THETA = 10000.0
EPS = 1e-6
NEG = -1e30

NSA_Q_COLS = NSA_HEADS * NSA_HEAD_DIM
NSA_KV_COLS = NSA_KV_GROUPS * NSA_HEAD_DIM
NSA_GATE_COLS = 3 * NSA_HEADS
HG_K_COLS = HG_HEADS * HG_DK
HG_V_COLS = HG_HEADS * HG_DV
N_IN = NSA_Q_COLS + 6 * NSA_KV_COLS + NSA_GATE_COLS + 2 * HG_K_COLS + 2 * HG_V_COLS

kernel_name = "nsa_hgrn2_macaron_hybrid"


def rmsnorm(x, g):
    xf = x.astype(jnp.float32)
    y = xf * lax.rsqrt(jnp.mean(xf * xf, axis=-1, keepdims=True) + EPS) * g.astype(jnp.float32)
    return y.astype(x.dtype)


def swiglu(x, w_gate, w_up, w_down):
    return (jax.nn.silu(x @ w_gate) * (x @ w_up)) @ w_down


def rope_tables(seq, dim):
    inv = 1.0 / (ROPE_THETA ** (jnp.arange(0, dim, 2, dtype=jnp.float32) / dim))
    ang = jnp.arange(seq, dtype=jnp.float32)[:, None] * inv[None, :]
    ang = jnp.concatenate([ang, ang], axis=-1)
    return jnp.cos(ang), jnp.sin(ang)


def norm_rope(x, g, cos, sin):
    xf = x.astype(jnp.float32)
    xf = xf * lax.rsqrt(jnp.mean(xf * xf, axis=-1, keepdims=True) + EPS) * g.astype(jnp.float32)
    half = xf.shape[-1] // 2
    rot = jnp.concatenate([-xf[..., half:], xf[..., :half]], axis=-1)
    y = xf * cos[None, :, None, :] + rot * sin[None, :, None, :]
    return y.astype(x.dtype)


def compress(t, idx, pos, w1, w2):
    blk = t[:, idx] + pos[None, None, :, None, :].astype(t.dtype)
    h = jax.nn.gelu(jnp.einsum("bnlgd,lde->bnge", blk, w1))
    return jnp.einsum("bnge,ed->bngd", h, w2)


def nsa_group(q, k_cmp, v_cmp, k_sel, v_sel, k_win, v_win, gates, cos, sin,
              q_gain, k_gain, cmp_pos, cmp_w1, cmp_w2):
    B, S = q.shape[0], q.shape[1]
    G, R, dh = NSA_KV_GROUPS, NSA_REP, NSA_HEAD_DIM
    dt = q.dtype
    scale = dh ** -0.5
    q = norm_rope(q, q_gain, cos, sin)
    kc = norm_rope(k_cmp, k_gain[0], cos, sin)
    ks = norm_rope(k_sel, k_gain[1], cos, sin)
    kw = norm_rope(k_win, k_gain[2], cos, sin)

    ncb = (S - CMP_BLOCK) // CMP_STRIDE + 1
    idx = jnp.arange(ncb)[:, None] * CMP_STRIDE + jnp.arange(CMP_BLOCK)[None, :]
    kc_c = compress(kc, idx, cmp_pos[0], cmp_w1[0], cmp_w2[0])
    vc_c = compress(v_cmp, idx, cmp_pos[1], cmp_w1[1], cmp_w2[1])
    cmp_end = jnp.arange(ncb) * CMP_STRIDE + CMP_BLOCK - 1

    nsb = S // SEL_BLOCK
    n_sel = min(SEL_TOPN, nsb)
    ks_blk = ks.reshape(B, nsb, SEL_BLOCK, G, dh).transpose(0, 3, 1, 2, 4)
    vs_blk = v_sel.reshape(B, nsb, SEL_BLOCK, G, dh).transpose(0, 3, 1, 2, 4)
    ci = jnp.arange(ncb)[:, None]
    sj = jnp.arange(nsb)[None, :]
    overlap = ((ci * CMP_STRIDE <= sj * SEL_BLOCK + SEL_BLOCK - 1)
               & (ci * CMP_STRIDE + CMP_BLOCK - 1 >= sj * SEL_BLOCK)).astype(jnp.float32)

    kw_pad = jnp.pad(kw, ((0, 0), (WINDOW, 0), (0, 0), (0, 0)))
    vw_pad = jnp.pad(v_win, ((0, 0), (WINDOW, 0), (0, 0), (0, 0)))
    bi = jnp.arange(B)[:, None, None, None]
    gi = jnp.arange(G)[None, :, None, None]
    jblk = jnp.arange(nsb)

    def block_fn(bq):
        start = bq * Q_BLOCK
        t = start + jnp.arange(Q_BLOCK)
        qb = lax.dynamic_slice_in_dim(q, start, Q_BLOCK, axis=1).reshape(B, Q_BLOCK, G, R, dh)
        gb = lax.dynamic_slice_in_dim(gates, start, Q_BLOCK, axis=1).reshape(B, Q_BLOCK, G, R, 3)

        s1 = jnp.einsum("bqgrd,bngd->bgrqn", qb, kc_c).astype(jnp.float32) * scale
        m1 = cmp_end[None, :] <= t[:, None]
        p1 = jnp.where(m1, jax.nn.softmax(jnp.where(m1, s1, NEG), axis=-1), 0.0)
        o_c = jnp.einsum("bgrqn,bngd->bqgrd", p1.astype(dt), vc_c)

        imp = jnp.einsum("bgrqn,nj->bgqj", p1, overlap)
        cur = t // SEL_BLOCK
        forced = (jblk[None, :] == 0) | (jblk[None, :] == cur[:, None]) | (jblk[None, :] == cur[:, None] - 1)
        valid = jblk[None, :] * SEL_BLOCK <= t[:, None]
        imp = jnp.where(valid, jnp.where(forced, jnp.inf, imp), -jnp.inf)
        _, sel = lax.top_k(imp, n_sel)
        kg = ks_blk[bi, gi, sel]
        vg = vs_blk[bi, gi, sel]
        s2 = jnp.einsum("bqgrd,bgqnld->bgrqnl", qb, kg).astype(jnp.float32) * scale
        kpos = sel[..., None] * SEL_BLOCK + jnp.arange(SEL_BLOCK)
        m2 = (kpos <= t[None, None, :, None, None])[:, :, None]
        s2 = jnp.where(m2, s2, NEG)
        p2 = jax.nn.softmax(s2.reshape(s2.shape[:4] + (-1,)), axis=-1).reshape(s2.shape)
        o_s = jnp.einsum("bgrqnl,bgqnld->bqgrd", p2.astype(dt), vg)

        kwb = lax.dynamic_slice_in_dim(kw_pad, start, Q_BLOCK + WINDOW, axis=1)
        vwb = lax.dynamic_slice_in_dim(vw_pad, start, Q_BLOCK + WINDOW, axis=1)
        kp = start - WINDOW + jnp.arange(Q_BLOCK + WINDOW)
        diff = t[:, None] - kp[None, :]
        m3 = (diff >= 0) & (diff < WINDOW) & (kp[None, :] >= 0)
        s3 = jnp.einsum("bqgrd,bkgd->bgrqk", qb, kwb).astype(jnp.float32) * scale
        p3 = jax.nn.softmax(jnp.where(m3, s3, NEG), axis=-1)
        o_w = jnp.einsum("bgrqk,bkgd->bqgrd", p3.astype(dt), vwb)

        o = gb[..., 0:1] * o_c + gb[..., 1:2] * o_s + gb[..., 2:3] * o_w
        return o.reshape(B, Q_BLOCK, NSA_HEADS * dh)

    out = lax.map(block_fn, jnp.arange(S // Q_BLOCK))
    return out.transpose(1, 0, 2, 3).reshape(B, S, NSA_HEADS * dh)


def hgrn2_group(hq, hf, hi, hg, lb, out_gain):
    B, S = hq.shape[0], hq.shape[1]
    H, dk, dv, C = HG_HEADS, HG_DK, HG_DV, S // HG_CHUNK
    dt = hq.dtype
    q = jax.nn.silu(hq.astype(jnp.float32)).reshape(B, S, H, dk)
    f = lb[None, None] + (1.0 - lb[None, None]) * jax.nn.sigmoid(hf.astype(jnp.float32).reshape(B, S, H, dk))
    logf = jnp.log(jnp.maximum(f, 1e-30))
    k = 1.0 - f
    v = hi.astype(jnp.float32).reshape(B, S, H, dv)

    def to_chunks(a):
        return a.reshape(B, C, HG_CHUNK, H, a.shape[-1]).transpose(1, 0, 3, 2, 4)

    tri = jnp.tril(jnp.ones((HG_CHUNK, HG_CHUNK), dtype=bool))

    def step(state, xs):
        qc, kc, vc, lfc = xs
        b = jnp.cumsum(lfc, axis=-2)
        decay = jnp.exp(jnp.where(tri[:, :, None], b[..., :, None, :] - b[..., None, :, :], -jnp.inf))
        a = jnp.einsum("bhtk,bhtsk,bhsk->bhts", qc, decay, kc)
        o = jnp.einsum("bhts,bhsv->bhtv", a, vc) + jnp.einsum("bhtk,bhkv->bhtv", qc * jnp.exp(b), state)
        bl = b[..., -1:, :]
        state = jnp.exp(bl)[..., 0, :, None] * state + jnp.einsum("bhsk,bhsv->bhkv", kc * jnp.exp(bl - b), vc)
        return state, o

    s0 = jnp.zeros((B, H, dk, dv), jnp.float32)
    _, o = lax.scan(step, s0, (to_chunks(q), to_chunks(k), to_chunks(v), to_chunks(logf)))
    o = o.transpose(1, 0, 3, 2, 4).reshape(B, S, H, dv)
    o = o * lax.rsqrt(jnp.mean(o * o, axis=-1, keepdims=True) + EPS) * out_gain.astype(jnp.float32)
    o = o * jax.nn.silu(hg.astype(jnp.float32).reshape(B, S, H, dv))
    return o.reshape(B, S, H * dv).astype(dt)


def setup_inputs(seed: int = 0) -> dict:
    key = jax.random.key(seed)
    ks = jax.random.split(key, 20)
    L, D, F, dh = DEPTH, D_MODEL, D_FF, NSA_HEAD_DIM

    def nrm(k, shape, scale):
        return jax.random.normal(k, shape, jnp.float32) * scale

    def gain(k, shape):
        return 1.0 + 0.01 * jax.random.normal(k, shape, jnp.float32)

    return {
        "x": nrm(ks[0], (BATCH, SEQ, D), 1.0),
        "ffn1_norm": gain(ks[1], (L, D)),
        "ffn1_w_gate": nrm(ks[2], (L, D, F), D ** -0.5),
        "ffn1_w_up": nrm(ks[3], (L, D, F), D ** -0.5),
        "ffn1_w_down": nrm(ks[4], (L, F, D), F ** -0.5),
        "mix_norm": gain(ks[5], (L, D)),
        "w_in": nrm(ks[6], (L, D, N_IN), D ** -0.5),
        "q_norm": gain(ks[7], (L, dh)),
        "k_norm": gain(ks[8], (L, 3, dh)),
        "cmp_pos": nrm(ks[9], (L, 2, CMP_BLOCK, dh), 0.02),
        "cmp_w1": nrm(ks[10], (L, 2, CMP_BLOCK, dh, CMP_HIDDEN), (CMP_BLOCK * dh) ** -0.5),
        "cmp_w2": nrm(ks[11], (L, 2, CMP_HIDDEN, dh), CMP_HIDDEN ** -0.5),
        "hgrn_lb_logits": nrm(ks[12], (L, HG_HEADS * HG_DK), 0.5),
        "hgrn_out_norm": gain(ks[13], (L, HG_DV)),
        "w_out": nrm(ks[14], (L, D_MIX, D), D_MIX ** -0.5),
        "ffn2_norm": gain(ks[15], (L, D)),
        "ffn2_w_gate": nrm(ks[16], (L, D, F), D ** -0.5),
        "ffn2_w_up": nrm(ks[17], (L, D, F), D ** -0.5),
        "ffn2_w_down": nrm(ks[18], (L, F, D), F ** -0.5),
    }


def reference(x, ffn1_norm, ffn1_w_gate, ffn1_w_up, ffn1_w_down, mix_norm, w_in, q_norm, k_norm,
              cmp_pos, cmp_w1, cmp_w2, hgrn_lb_logits, hgrn_out_norm, w_out,
              ffn2_norm, ffn2_w_gate, ffn2_w_up, ffn2_w_down):
    B, S = x.shape[0], x.shape[1]
    cos, sin = rope_tables(S, NSA_HEAD_DIM)
    lb_sm = jax.nn.softmax(hgrn_lb_logits.astype(jnp.float32), axis=0)
    lb_all = jnp.cumsum(lb_sm, axis=0) - lb_sm[0:1]
    sizes = [NSA_Q_COLS] + [NSA_KV_COLS] * 6 + [NSA_GATE_COLS, HG_K_COLS, HG_K_COLS, HG_V_COLS, HG_V_COLS]
    offsets = [int(o) for o in np.cumsum(sizes)[:-1]]

    for l in range(DEPTH):
        x = x + 0.5 * swiglu(rmsnorm(x, ffn1_norm[l]), ffn1_w_gate[l], ffn1_w_up[l], ffn1_w_down[l])
        h = rmsnorm(x, mix_norm[l])
        proj = h @ w_in[l]
        (q, kc, vc, ksl, vsl, kw, vw, gts, hq, hf, hi, hg) = jnp.split(proj, offsets, axis=-1)
        kv = lambda a: a.reshape(B, S, NSA_KV_GROUPS, NSA_HEAD_DIM)
        gates = jax.nn.sigmoid(gts.astype(jnp.float32)).reshape(B, S, NSA_HEADS, 3).astype(x.dtype)
        o_nsa = nsa_group(q.reshape(B, S, NSA_HEADS, NSA_HEAD_DIM), kv(kc), kv(vc), kv(ksl), kv(vsl),
                          kv(kw), kv(vw), gates, cos, sin, q_norm[l], k_norm[l],
                          cmp_pos[l], cmp_w1[l], cmp_w2[l])
        o_hg = hgrn2_group(hq, hf, hi, hg, lb_all[l].reshape(HG_HEADS, HG_DK), hgrn_out_norm[l])
        x = x + jnp.concatenate([o_nsa, o_hg], axis=-1) @ w_out[l]
        x = x + 0.5 * swiglu(rmsnorm(x, ffn2_norm[l]), ffn2_w_gate[l], ffn2_w_up[l], ffn2_w_down[l])
    return x
```

```python
import numpy as np
import ml_dtypes
import concourse.bass as bass
import concourse.mybir as mybir
from concourse.bass_utils import run_bass_kernel_spmd

F32 = mybir.dt.float32
BF16 = mybir.dt.bfloat16
AF = mybir.ActivationFunctionType
ALU = mybir.AluOpType
AX = mybir.AxisListType

ENGS = ("pe", "act", "dve", "pool", "sp")
D = 1024
FF = 2816
NCORE = 8
TPC = 2048
NT = 16
SEQ = 4096
NQT = 32
EPS = 1e-6
NIN = 1676


class Res:
    __slots__ = ("name", "last_w", "readers", "excl")

    def __init__(self, name, excl=False, after=None):
        self.name = name
        self.last_w = None
        self.readers = list(after) if after else []
        self.excl = excl


class Op:
    __slots__ = ("eng", "fn", "r", "w", "dma", "deps", "sig", "waits", "need", "idx")

    def __init__(self, eng, fn, r, w, dma):
        self.eng, self.fn, self.r, self.w, self.dma = eng, fn, r, w, dma
        self.deps = set()
        self.sig = None
        self.waits = []
        self.need = False


class Prog:
    def __init__(self, nc, same_engine_sync=True, ndma_sems=8):
        self.nc = nc
        self.ops = []
        self.same_engine_sync = same_engine_sync
        self.ndma_sems = ndma_sems
        self._ctx = []
        self.last_op = {}
        self.last_dmas = {e: [] for e in ENGS}

    def _enter(self, cm):
        v = cm.__enter__()
        self._ctx.append(cm)
        return v

    def sbuf(self, name, shape, dtype):
        return self._enter(self.nc.sbuf_tensor(name, list(shape), dtype))

    def psum(self, name, shape, dtype):
        return self._enter(self.nc.psum_tensor(name, list(shape), dtype))

    def fence(self):
        f = list(self.last_op.values())
        for e in ENGS:
            f.extend(self.last_dmas[e])
        return f

    def res(self, name, excl=False, after=None):
        return Res(name, excl, after)

    def op(self, eng, fn, r=(), w=(), dma=False):
        o = Op(eng, fn, [x for x in r if x is not None], [x for x in w if x is not None], dma)
        i = len(self.ops)
        self.ops.append(o)
        if dma:
            ld = self.last_dmas[eng]
            ld.append(i)
            if len(ld) > self.ndma_sems:
                ld.pop(0)
        else:
            self.last_op[eng] = i
        return o

    def pe(self, fn, r=(), w=()):
        return self.op("pe", fn, r, w)

    def act(self, fn, r=(), w=()):
        return self.op("act", fn, r, w)

    def dve(self, fn, r=(), w=()):
        return self.op("dve", fn, r, w)

    def pool(self, fn, r=(), w=()):
        return self.op("pool", fn, r, w)

    def dma(self, eng, fn, r=(), w=()):
        return self.op(eng, fn, r, w, dma=True)

    def build(self):
        nc = self.nc
        ops = self.ops
        for i, o in enumerate(ops):
            o.idx = i
            deps = set()
            rs = [x for x in o.r if not x.excl]
            ws = list(o.w) + [x for x in o.r if x.excl]
            for x in rs:
                if x.last_w is not None:
                    deps.add(x.last_w)
            for x in ws:
                if x.last_w is not None:
                    deps.add(x.last_w)
                deps.update(x.readers)
            for x in rs:
                x.readers.append(i)
            for x in ws:
                x.last_w = i
                x.readers = []
            deps.discard(i)
            for j in deps:
                p = ops[j]
                if p.eng == o.eng and not p.dma and not o.dma:
                    if o.eng == "pe" or not self.same_engine_sync:
                        continue
                o.deps.add(j)
                p.need = True
        sems = {e: self._enter(nc.semaphore("c_" + e)) for e in ENGS}
        dsems = {e: [self._enter(nc.semaphore("d_%s%d" % (e, k))) for k in range(self.ndma_sems)]
                 for e in ("sp", "pool", "act")}
        cnt = {e: 0 for e in ENGS}
        dcnt = {e: 0 for e in ENGS}
        waited = {e: {} for e in ENGS}
        final_waits = {e: {} for e in ENGS}
        for o in ops:
            w = {}
            for j in o.deps:
                s, v = ops[j].sig
                if w.get(s, 0) < v:
                    w[s] = v
            if o.dma:
                k = dcnt[o.eng]
                dcnt[o.eng] += 1
                s = dsems[o.eng][k % self.ndma_sems]
                prev = 16 * (k // self.ndma_sems)
                if prev > 0 and w.get(s, 0) < prev:
                    w[s] = prev
                o.sig = (s, prev + 16)
                final_waits[o.eng][s] = prev + 16
            elif o.need:
                cnt[o.eng] += 1
                o.sig = (sems[o.eng], cnt[o.eng])
            wd = waited[o.eng]
            o.waits = []
            for s, v in w.items():
                key = id(s)
                if wd.get(key, 0) < v:
                    wd[key] = v
                    o.waits.append((s, v))
        self.stats = {e: sum(1 for o in ops if o.eng == e) for e in ENGS}
        self.stats["sem_counts"] = dict(cnt)
        with nc.Block() as block:
            def emit(ename, eng):
                for o in ops:
                    if o.eng != ename:
                        continue
                    for s, v in o.waits:
                        eng.wait_ge(s, v)
                    ins = o.fn(eng)
                    if o.sig is not None:
                        ins.then_inc(o.sig[0], 16 if o.dma else 1)
                for s, v in final_waits[ename].items():
                    eng.wait_ge(s, v)

            @block.tensor
            def _(e):
                emit("pe", e)

            @block.scalar
            def _(e):
                emit("act", e)

            @block.vector
            def _(e):
                emit("dve", e)

            @block.gpsimd
            def _(e):
                emit("pool", e)

            @block.sync
            def _(e):
                emit("sp", e)
        for cm in reversed(self._ctx):
            cm.__exit__(None, None, None)
        self._ctx = []


ARENA_BYTES = 172 * 1024
X_BYTES = 64 * 1024


class KB:
    def __init__(self, nc):
        self.nc = nc
        self.P = P = Prog(nc)
        self.arena = P.sbuf("arena", [128, ARENA_BYTES // 4], F32)
        self.X = self.arena[:, 0:X_BYTES // 4].rearrange("p (a b) -> p a b", a=NT)
        self.rX = [P.res("X%d" % i) for i in range(NT)]
        self.aoff = X_BYTES
        self.fence = []
        self.pb = [P.psum("pb%d" % i, [128, 512], F32) for i in range(8)]
        self.rpb = [P.res("pb%d" % i, excl=True) for i in range(8)]
        self.identf = P.sbuf("identf", [128, 128], F32)
        self.identb = P.sbuf("identb", [128, 128], BF16)
        self.r_identf = P.res("identf")
        self.r_identb = P.res("identb")
        P.pool(lambda e: e.memset(self.identf[:], 1.0), w=[self.r_identf])
        P.pool(lambda e: e.affine_select(out=self.identf[:], in_=self.identf[:], pattern=[[-1, 128]],
                                         compare_op=ALU.is_equal, fill=0.0, base=0, channel_multiplier=1),
               r=[self.r_identf], w=[self.r_identf])
        P.dve(lambda e: e.tensor_copy(out=self.identb[:], in_=self.identf[:]), r=[self.r_identf], w=[self.r_identb])

    def areset(self, keep_x=True):
        self.aoff = X_BYTES if keep_x else 0
        self.fence = self.P.fence()

    def new_X(self):
        self.fence = self.P.fence()
        self.rX = [self.res("X%d" % i) for i in range(NT)]

    def alloc(self, shape, dtype):
        esz = 2 if dtype == BF16 else 4
        n = 1
        for s in shape[1:]:
            n *= s
        nbytes = (n * esz + 63) // 64 * 64
        off = self.aoff
        self.aoff += nbytes
        assert self.aoff <= ARENA_BYTES, ("arena overflow", self.aoff)
        v = self.arena[:, off // 4:(off + nbytes) // 4]
        if dtype == BF16:
            v = v.bitcast(BF16)
        v = v[:, 0:n]
        if len(shape) == 3:
            v = v.rearrange("p (a b) -> p a b", a=shape[1])
        elif len(shape) == 4:
            v = v.rearrange("p (a b c) -> p a b c", a=shape[1], b=shape[2])
        return v

    def res(self, name):
        return self.P.res(name, after=self.fence)

    def bankbf(self, i):
        return self.pb[i][:].bitcast(BF16)

    def norm_transpose(self, g_row, xnT, r_xnT, tag):
        P = self.P
        gbc = self.alloc([128, D], F32); r_gbc = self.res("gbc" + tag)
        xnb = [self.alloc([128, D], BF16) for _ in range(2)]
        r_xnb = [self.res("xnb%d%s" % (i, tag)) for i in range(2)]
        junk = self.alloc([128, D], BF16); r_junk = self.res("junk" + tag)
        ss = self.alloc([128, NT], F32); r_ss = self.res("ss" + tag)
        rstd = self.alloc([128, NT], F32); r_rstd = self.res("rstd" + tag)
        X, rX = self.X, self.rX
        P.dma("sp", lambda e: e.dma_start(out=gbc, in_=g_row.partition_broadcast(128)), w=[r_gbc])
        for i in range(NT):
            P.act(lambda e, i=i: e.activation(out=junk, in_=X[:, i, :], func=AF.Square, accum_out=ss[:, i:i + 1]),
                  r=[rX[i]], w=[r_junk, r_ss])
        P.act(lambda e: e.activation(out=rstd, in_=ss, func=AF.Sqrt, scale=1.0 / D, bias=EPS), r=[r_ss], w=[r_rstd])
        P.dve(lambda e: e.reciprocal(out=rstd, in_=rstd), r=[r_rstd], w=[r_rstd])
        for i in range(NT):
            b = i % 2
            P.dve(lambda e, i=i, b=b: e.scalar_tensor_tensor(out=xnb[b], in0=X[:, i, :], scalar=rstd[:, i:i + 1],
                                                             in1=gbc, op0=ALU.mult, op1=ALU.mult),
                  r=[rX[i], r_rstd, r_gbc], w=[r_xnb[b]])
            bank = 6 + b
            pbt = self.bankbf(bank)
            for kc in range(8):
                P.pe(lambda e, kc=kc, b=b, pbt=pbt: e.transpose(out=pbt[:, kc * 128:(kc + 1) * 128],
                                                               in_=xnb[b][:, kc * 128:(kc + 1) * 128],
                                                               identity=self.identb[:]),
                     r=[r_xnb[b], self.r_identb], w=[self.rpb[bank]])
            src = pbt.rearrange("p (k t) -> p k t", k=8)
            dst = xnT[:, :, i * 128:(i + 1) * 128]
            if b == 0:
                P.act(lambda e, src=src, dst=dst: e.activation(out=dst, in_=src, func=AF.Copy),
                      r=[self.rpb[bank]], w=[r_xnT[i]])
            else:
                P.dve(lambda e, src=src, dst=dst: e.tensor_copy(out=dst, in_=src), r=[self.rpb[bank]], w=[r_xnT[i]])

    def ffn(self, g_row, wg, wu, wd, tag):
        P = self.P
        self.areset()
        xnT = self.alloc([128, 8, TPC], BF16)
        r_xnT = [self.res("xnT%d%s" % (i, tag)) for i in range(NT)]
        self.norm_transpose(g_row, xnT, r_xnT, tag)
        wgb = [self.alloc([128, 8, 256], BF16) for _ in range(2)]
        wub = [self.alloc([128, 8, 256], BF16) for _ in range(2)]
        wdb = [self.alloc([128, 2, D], BF16) for _ in range(2)]
        r_wg = [self.res("wg%d%s" % (i, tag)) for i in range(2)]
        r_wu = [self.res("wu%d%s" % (i, tag)) for i in range(2)]
        r_wd = [self.res("wd%d%s" % (i, tag)) for i in range(2)]
        sil = [self.alloc([128, 256], F32) for _ in range(2)]
        r_sil = [self.res("sil%d%s" % (i, tag)) for i in range(2)]
        actb = [self.alloc([128, 2, 256], BF16) for _ in range(2)]
        r_actb = [[self.res("actb%d_%d%s" % (i, fc, tag)) for fc in range(2)] for i in range(2)]
        wgv = wg.rearrange("(kc p) f -> p kc f", p=128)
        wuv = wu.rearrange("(kc p) f -> p kc f", p=128)
        wdv = wd.rearrange("(fc p) d -> p fc d", p=128)
        NG = FF // 256
        X, rX, pb, rpb = self.X, self.rX, self.pb, self.rpb

        def load_group(gi):
            b = gi % 2
            f0 = gi * 256
            P.dma("pool", lambda e: e.dma_start(out=wgb[b], in_=wgv[:, :, f0:f0 + 256]), w=[r_wg[b]])
            P.dma("pool", lambda e: e.dma_start(out=wub[b], in_=wuv[:, :, f0:f0 + 256]), w=[r_wu[b]])
            P.dma("pool", lambda e: e.dma_start(out=wdb[b], in_=wdv[:, 2 * gi:2 * gi + 2, :]), w=[r_wd[b]])

        def emit_y(gi, tb, ab):
            b = gi % 2
            for tt in range(2):
                for dh in range(2):
                    bank = tt * 2 + dh
                    for fc in range(2):
                        P.pe(lambda e, bank=bank, fc=fc, tt=tt, dh=dh: e.matmul(
                            pb[bank][:], lhsT=actb[ab][:, fc, tt * 128:(tt + 1) * 128],
                            rhs=wdb[b][:, fc, dh * 512:(dh + 1) * 512], start=(fc == 0), stop=(fc == 1)),
                            r=[r_actb[ab][fc], r_wd[b]], w=[rpb[bank]])
            for tt in range(2):
                for dh in range(2):
                    bank = tt * 2 + dh
                    ti = tb * 2 + tt
                    xs = X[:, ti, dh * 512:(dh + 1) * 512]
                    P.dve(lambda e, bank=bank, xs=xs: e.scalar_tensor_tensor(out=xs, in0=pb[bank][:], scalar=0.5, in1=xs,
                                                                             op0=ALU.mult, op1=ALU.add),
                          r=[rpb[bank], rX[ti]], w=[rX[ti]])

        load_group(0)
        pending = None
        it = 0
        for gi in range(NG):
            b = gi % 2
            for tb in range(8):
                ab = it % 2
                for fc in range(2):
                    bank = 4 + fc
                    for kc in range(8):
                        P.pe(lambda e, bank=bank, kc=kc, fc=fc, tb=tb, b=b: e.matmul(
                            pb[bank][:, 0:256], lhsT=wgb[b][:, kc, fc * 128:(fc + 1) * 128],
                            rhs=xnT[:, kc, tb * 256:(tb + 1) * 256], start=(kc == 0), stop=(kc == 7)),
                            r=[r_wg[b], r_xnT[2 * tb], r_xnT[2 * tb + 1]], w=[rpb[bank]])
                    for kc in range(8):
                        P.pe(lambda e, bank=bank, kc=kc, fc=fc, tb=tb, b=b: e.matmul(
                            pb[bank][:, 256:512], lhsT=wub[b][:, kc, fc * 128:(fc + 1) * 128],
                            rhs=xnT[:, kc, tb * 256:(tb + 1) * 256], start=(kc == 0), stop=(kc == 7)),
                            r=[r_wu[b], r_xnT[2 * tb], r_xnT[2 * tb + 1]], w=[rpb[bank]])
                    P.act(lambda e, bank=bank, fc=fc: e.activation(out=sil[fc], in_=pb[bank][:, 0:256], func=AF.Silu),
                          r=[rpb[bank]], w=[r_sil[fc]])
                    P.dve(lambda e, bank=bank, fc=fc, ab=ab: e.tensor_tensor(out=actb[ab][:, fc, :], in0=sil[fc],
                                                                            in1=pb[bank][:, 256:512], op=ALU.mult),
                          r=[r_sil[fc], rpb[bank]], w=[r_actb[ab][fc]])
                if pending is not None:
                    emit_y(*pending)
                if tb == 0 and gi + 1 < NG:
                    load_group(gi + 1)
                pending = (gi, tb, ab)
                it += 1
        emit_y(*pending)

    def load_X(self, x_dram):
        P = self.P
        self.new_X()
        xv = x_dram.rearrange("(i p) d -> p i d", p=128)
        for i in range(NT):
            P.dma("sp", lambda e, i=i: e.dma_start(out=self.X[:, i, :], in_=xv[:, i, :]), w=[self.rX[i]])

    def store_X(self, y_dram):
        P = self.P
        yv = y_dram.rearrange("(i p) d -> p i d", p=128)
        for i in range(NT):
            P.dma("sp", lambda e, i=i: e.dma_start(out=yv[:, i, :], in_=self.X[:, i, :]), r=[self.rX[i]])

    def emit_h(self, g_row, hx_dram, tag):
        P = self.P
        self.areset()
        hT = self.alloc([128, 8, TPC], BF16)
        r_hT = [self.res("hT%d%s" % (i, tag)) for i in range(NT)]
        self.norm_transpose(g_row, hT, r_hT, tag)
        for i in range(NT):
            P.dma("sp", lambda e, i=i: e.dma_start(out=hx_dram[i].rearrange("p (k t) -> p k t", k=8),
                                                   in_=hT[:, :, i * 128:(i + 1) * 128]), r=[r_hT[i]])


def _dram_in(nc, name, shape, dtype=F32):
    return nc.dram_tensor(name, list(shape), dtype, kind="ExternalInput").ap()


def _dram_out(nc, name, shape, dtype=F32):
    return nc.dram_tensor(name, list(shape), dtype, kind="ExternalOutput").ap()


def build_phaseA(do_h=True):
    nc = bass.Bass("TRN2", target_bir_lowering=False)
    x = _dram_in(nc, "x", [TPC, D])
    g1 = _dram_in(nc, "g1", [1, D])
    wg = _dram_in(nc, "wg", [D, FF])
    wu = _dram_in(nc, "wu", [D, FF])
    wd = _dram_in(nc, "wd", [FF, D])
    gm = _dram_in(nc, "gm", [1, D])
    y = _dram_out(nc, "y", [TPC, D])
    hx = _dram_out(nc, "hx", [NT, 128, D], BF16)
    kb = KB(nc)
    kb.load_X(x)
    kb.ffn(g1, wg, wu, wd, "a")
    kb.store_X(y)
    if do_h:
        kb.emit_h(gm, hx, "h")
    kb.P.build()
    return nc, kb


CST_W0, CST_C, CST_HTRI, CST_BLK, CST_BW, CST_CHK, CST_E0, CST_N = 0, 128, 256, 384, 512, 640, 642, 706


def host_consts():
    k = np.arange(128)[:, None]
    q = np.arange(128)[None, :]
    c = np.zeros((128, CST_N), np.float32)
    c[:, CST_W0:CST_W0 + 128] = (q < k)
    c[:, CST_C:CST_C + 128] = (q >= k)
    c[:, CST_HTRI:CST_HTRI + 128] = (k <= q) & (k // 64 == q // 64)
    c[:, CST_BLK:CST_BLK + 128] = (k // 64 == q // 64)
    ql = np.arange(128)[:, None]
    rel = np.arange(128)[None, :] - 64
    cc = (ql >= 64).astype(np.int64)
    bw = np.zeros((128, 128), np.float32)
    bw[(rel == cc) | (rel == cc - 1)] = 1e9
    bw[rel > cc] = -1.0
    c[:, CST_BW:CST_BW + 128] = bw
    c[:, CST_CHK:CST_CHK + 2] = (np.arange(128)[:, None] // 64 == np.arange(2)[None, :])
    c[:, CST_E0] = 1e9
    return c


def host_rope():
    inv = (1.0 / (np.float32(10000.0) ** (np.arange(0, 64, 2, dtype=np.float32) / np.float32(64)))).astype(np.float32)
    ang = (np.arange(SEQ, dtype=np.float32)[:, None] * inv[None, :]).astype(np.float32)
    ang = np.concatenate([ang, ang], axis=-1)
    cos = np.cos(ang).astype(np.float32)
    sin = np.sin(ang).astype(np.float32)
    sinS = sin.copy()
    sinS[:, :32] = -sinS[:, :32]
    return np.ascontiguousarray(np.concatenate([cos, sinS], axis=1))


def host_overlap():
    ci = np.arange(256)[:, None]
    sj = np.arange(64)[None, :]
    ov = ((ci * 16 <= sj * 64 + 63) & (ci * 16 + 31 >= sj * 64)).astype(np.float32)
    ov[255] = 0.0
    return ov


class Mixer:
    def __init__(self, kb, layer, io):
        self.kb = kb
        self.l = layer
        self.io = io

    def run(self):
        kb = self.kb
        P = kb.P
        io = self.io
        l = self.l
        kb.areset(keep_x=False)
        A = kb.alloc
        R = kb.res
        pb, rpb = kb.pb, kb.rpb
        winb = A([128, 8, NIN], BF16); r_win = R("winb")
        CACHE = A([128, 5, SEQ], BF16)
        QC = A([128, NQT, 2, 128], BF16)
        r_cache = [R("cache%d" % t) for t in range(NQT)]
        VAUG = A([128, NQT, 2, 66], BF16)
        r_vaug = [R("vaug%d" % t) for t in range(NQT)]
        r_vones = R("vones")
        GATES = A([128, NQT, 12], F32)
        r_gates = [R("gates%d" % t) for t in range(NQT)]
        cst = A([128, CST_N], F32); r_cst = R("cst")
        cstb = A([128, 644], BF16); r_cstb = R("cstb")
        gqk = A([128, 7, 64], F32); r_gqk = R("gqk")
        lbc = A([128, 256], F32); omlb = A([128, 256], F32); r_lb = R("lb")
        ogn = A([128, 256], F32); r_ogn = R("ogn")
        S = A([128, 2, 128], F32); r_S = [R("S0"), R("S1")]
        Sbf = A([128, 2, 128], BF16); r_Sbf = [R("Sbf0"), R("Sbf1")]
        OUT = [A([128, 512], BF16) for _ in range(2)]
        r_out = [R("out0"), R("out1")]
        HGO = [A([128, 256], BF16) for _ in range(2)]
        r_hgo = [R("hgo0"), R("hgo1")]

        P.dma("pool", lambda e: e.dma_start(out=winb, in_=io["win"].rearrange("(kc p) n -> p kc n", p=128)), w=[r_win])
        P.dma("sp", lambda e: e.dma_start(out=cst, in_=io["cst"]), w=[r_cst])
        P.dve(lambda e: e.tensor_copy(out=cstb[:, 0:644], in_=cst[:, 0:644]), r=[r_cst], w=[r_cstb])
        P.dma("sp", lambda e: e.dma_start(out=gqk.rearrange("p a b -> p (a b)"), in_=io["gqk"].partition_broadcast(128)),
              w=[r_gqk])
        P.dma("sp", lambda e: e.dma_start(out=ogn[:, 0:128], in_=io["ogn"].partition_broadcast(128)), w=[r_ogn])
        P.dma("sp", lambda e: e.dma_start(out=ogn[:, 128:256], in_=io["ogn"].partition_broadcast(128)), w=[r_ogn])
        if l == 0:
            P.pool(lambda e: e.memset(lbc, 0.0), w=[r_lb])
            P.pool(lambda e: e.memset(omlb, 1.0), w=[r_lb])
        else:
            P.dma("sp", lambda e: e.dma_start(out=lbc, in_=io["lbl"][1:2, :].partition_broadcast(128)), w=[r_lb])
            P.dma("sp", lambda e: e.dma_start(out=omlb, in_=io["lbl"][0:1, :].partition_broadcast(128)), w=[r_lb])
            P.dve(lambda e: e.tensor_tensor(out=lbc, in0=lbc, in1=omlb, op=ALU.subtract), r=[r_lb], w=[r_lb])
            P.act(lambda e: e.activation(out=lbc, in_=lbc, func=AF.Sigmoid), r=[r_lb], w=[r_lb])
            P.dve(lambda e: e.tensor_scalar(out=omlb, in0=lbc, scalar1=-1.0, scalar2=1.0, op0=ALU.mult, op1=ALU.add),
                  r=[r_lb], w=[r_lb])
        P.pool(lambda e: e.memset(S, 0.0), w=r_S)
        if getattr(self, 'ntiles', NQT) < NQT:
            P.pool(lambda e: e.memset(CACHE, 0.0), w=r_cache)
            P.pool(lambda e: e.memset(VAUG, 0.0), w=r_vaug + [r_vones])
        P.pool(lambda e: e.memset(Sbf, 0.0), w=r_Sbf)
        P.pool(lambda e: e.memset(VAUG[:, :, :, 64:66], 1.0), w=[r_vones])

        W0b = cstb[:, 0:128]
        Cb = cstb[:, 128:256]
        HTRIb = cstb[:, 256:384]
        BLKb = cstb[:, CST_BLK:CST_BLK + 128]
        CHKb = cstb[:, CST_CHK:CST_CHK + 2]

        ht = [A([128, 8, 128], BF16) for _ in range(2)]; r_ht = [R("ht0"), R("ht1")]
        cs = [A([128, 128], F32) for _ in range(2)]; r_cs = [R("cs0"), R("cs1")]
        sq = A([128, 448], F32); r_sq = R("sq")
        ssq = A([128, 8], F32); r_ssq = R("ssq")
        t1 = A([128, 7, 64], F32); r_t1 = R("t1")
        t2 = A([128, 7, 64], F32); r_t2 = R("t2")
        ra = A([128, 7, 64], F32); r_ra = R("ra")
        rb = A([128, 7, 64], F32); r_rb = R("rb")
        ytok = A([128, 10, 64], BF16); r_ytok = R("ytok")
        qs = A([128, 256], F32); r_qs = R("qs")
        ff = A([128, 256], F32); r_ff = R("ff")
        logf = A([128, 256], F32); r_logf = R("logf")
        lhl = A([128, 2, 256], BF16); r_lhl = R("lhl")
        kk = A([128, 256], F32); r_kk = R("kk")
        bs = A([128, 256], F32); r_bs = R("bs")
        d1 = A([128, 256], F32); r_d1 = R("d1")
        d2 = A([128, 256], F32); r_d2 = R("d2")
        ex = [A([128, 256], F32) for _ in range(4)]; r_ex = [R("ex%d" % i) for i in range(4)]
        hb = A([128, 4, 256], BF16); r_hb = [R("hb%d" % i) for i in range(4)]
        dec = A([128, 4], F32); r_dec = R("dec")
        vb = A([128, 256], BF16); r_vb = R("vb")
        gsil = A([128, 256], F32); r_gsil = R("gsil")
        hgT = A([128, 6, 128], BF16); r_hgT = R("hgT")
        aTm = A([128, 2, 128], BF16); r_aTm = R("aTm")
        oss = A([128, 2], F32); r_oss = R("oss")
        junk = A([128, 128], BF16); r_junk = R("junkm")

        hin = io["hin"]
        csd = io["cs"]

        for t in range(getattr(self, 'ntiles', NQT)):
            hb_i = t % 2
            P.dma("sp", lambda e, t=t, hb_i=hb_i: e.dma_start(out=ht[hb_i], in_=hin[t].rearrange("p (k t) -> p k t", k=8)),
                  w=[r_ht[hb_i]])
            P.dma("sp", lambda e, t=t, hb_i=hb_i: e.dma_start(out=cs[hb_i], in_=csd[t * 128:(t + 1) * 128, :]), w=[r_cs[hb_i]])
            colr = [(0, 448), (448, 652), (652, 1164), (1164, 1676)]
            for bi, (c0, c1) in enumerate(colr):
                for kc in range(8):
                    P.pe(lambda e, bi=bi, c0=c0, c1=c1, kc=kc, hb_i=hb_i: e.matmul(
                        pb[bi][:, 0:c1 - c0], lhsT=ht[hb_i][:, kc, :], rhs=winb[:, kc, c0:c1],
                        start=(kc == 0), stop=(kc == 7)), r=[r_ht[hb_i], r_win], w=[rpb[bi]])
            b0v = pb[0][:, 0:448].rearrange("p (a b) -> p a b", a=7)
            P.act(lambda e: e.activation(out=sq, in_=pb[0][:, 0:448], func=AF.Square), r=[rpb[0]], w=[r_sq])
            P.dve(lambda e: e.tensor_reduce(out=ssq[:, 0:7], in_=sq.rearrange("p (a b) -> p a b", a=7), axis=AX.X, op=ALU.add),
                  r=[r_sq], w=[r_ssq])
            P.act(lambda e: e.activation(out=ssq[:, 0:7], in_=ssq[:, 0:7], func=AF.Sqrt, scale=1.0 / 64, bias=EPS),
                  r=[r_ssq], w=[r_ssq])
            P.dve(lambda e: e.reciprocal(out=ssq[:, 0:7], in_=ssq[:, 0:7]), r=[r_ssq], w=[r_ssq])
            P.dve(lambda e, b0v=b0v: e.tensor_tensor(out=t1, in0=b0v, in1=ssq[:, 0:7].unsqueeze(2).to_broadcast([128, 7, 64]),
                                                     op=ALU.mult), r=[rpb[0], r_ssq], w=[r_t1])
            P.pool(lambda e: e.tensor_tensor(out=t2, in0=t1, in1=gqk, op=ALU.mult), r=[r_t1, r_gqk], w=[r_t2])
            csb = cs[hb_i]
            P.pool(lambda e, csb=csb: e.tensor_tensor(out=ra, in0=t2, in1=csb[:, 0:64].unsqueeze(1).to_broadcast([128, 7, 64]),
                                                      op=ALU.mult), r=[r_t2, r_cs[hb_i]], w=[r_ra])
            P.pool(lambda e, csb=csb: e.tensor_tensor(out=rb[:, :, 0:32], in0=t2[:, :, 32:64],
                                                      in1=csb[:, 64:96].unsqueeze(1).to_broadcast([128, 7, 32]),
                                                      op=ALU.mult), r=[r_t2, r_cs[hb_i]], w=[r_rb])
            P.pool(lambda e, csb=csb: e.tensor_tensor(out=rb[:, :, 32:64], in0=t2[:, :, 0:32],
                                                      in1=csb[:, 96:128].unsqueeze(1).to_broadcast([128, 7, 32]),
                                                      op=ALU.mult), r=[r_t2, r_cs[hb_i]], w=[r_rb])
            P.dve(lambda e: e.tensor_tensor(out=ytok[:, 0:5, :], in0=ra[:, 0:5, :], in1=rb[:, 0:5, :], op=ALU.add),
                  r=[r_ra, r_rb], w=[r_ytok])
            yk = ytok[:, 6:10, :].rearrange("p (a b) d -> p a b d", b=2)
            for dup in range(2):
                P.dve(lambda e, dup=dup, yk=yk: e.tensor_tensor(out=yk[:, :, dup, :], in0=ra[:, 5:7, :], in1=rb[:, 5:7, :],
                                                                op=ALU.add), r=[r_ra, r_rb], w=[r_ytok])
            P.act(lambda e: e.activation(out=ytok[:, 5, :], in_=pb[1][:, 0:64], func=AF.Copy), r=[rpb[1]], w=[r_ytok])
            P.act(lambda e, t=t: e.activation(out=VAUG[:, t, :, 0:64], in_=pb[1][:, 64:192].rearrange("p (a b) -> p a b", a=2),
                                              func=AF.Copy), r=[rpb[1], r_vones], w=[r_vaug[t]])
            P.act(lambda e, t=t: e.activation(out=GATES[:, t, :], in_=pb[1][:, 192:204], func=AF.Sigmoid),
                  r=[rpb[1]], w=[r_gates[t]])
            b4 = kb.bankbf(4)
            for i in range(5):
                P.pe(lambda e, i=i, b4=b4: e.transpose(out=b4[:, i * 128:(i + 1) * 128],
                                                      in_=ytok[:, 2 * i:2 * i + 2, :].rearrange("p a b -> p (a b)"),
                                                      identity=kb.identb[:]), r=[r_ytok, kb.r_identb], w=[rpb[4]])
            P.dve(lambda e, t=t, b4=b4: e.tensor_copy(out=CACHE[:, 2:5, t * 128:(t + 1) * 128],
                                                      in_=b4[:, 256:640].rearrange("p (a b) -> p a b", a=3)),
                  r=[rpb[4]], w=[r_cache[t]])
            P.dve(lambda e, t=t, b4=b4: e.tensor_copy(out=QC[:, t, :, :].rearrange("p a b -> p (a b)"), in_=b4[:, 0:256]),
                  r=[rpb[4]], w=[r_cache[t]])
            P.act(lambda e: e.activation(out=qs, in_=pb[2][:, 0:256], func=AF.Silu), r=[rpb[2]], w=[r_qs])
            P.act(lambda e: e.activation(out=ff, in_=pb[2][:, 256:512], func=AF.Sigmoid), r=[rpb[2]], w=[r_ff])
            P.dve(lambda e: e.tensor_tensor(out=ff, in0=ff, in1=omlb, op=ALU.mult), r=[r_ff, r_lb], w=[r_ff])
            P.dve(lambda e: e.tensor_tensor(out=ff, in0=ff, in1=lbc, op=ALU.add), r=[r_ff, r_lb], w=[r_ff])
            P.act(lambda e: e.activation(out=logf, in_=ff, func=AF.Ln), r=[r_ff], w=[r_logf])
            P.dve(lambda e: e.tensor_scalar(out=kk, in0=ff, scalar1=-1.0, scalar2=1.0, op0=ALU.mult, op1=ALU.add),
                  r=[r_ff], w=[r_kk])
            P.act(lambda e: e.activation(out=lhl[:, 0, :], in_=logf, func=AF.Copy), r=[r_logf], w=[r_lhl])
            P.dve(lambda e: e.tensor_tensor(out=lhl[:, 1, :], in0=logf, in1=lhl[:, 0, :], op=ALU.subtract),
                  r=[r_logf, r_lhl], w=[r_lhl])
            for x in range(2):
                P.pe(lambda e, x=x: e.matmul(pb[5][:, 0:256], lhsT=HTRIb, rhs=lhl[:, x, :], start=(x == 0), stop=(x == 1)),
                     r=[r_cstb, r_lhl], w=[rpb[5]])
            for x in range(2):
                P.pe(lambda e, x=x: e.matmul(pb[5][:, 256:512], lhsT=BLKb, rhs=lhl[:, x, :], start=(x == 0), stop=(x == 1)),
                     r=[r_cstb, r_lhl], w=[rpb[5]])
            for h in range(2):
                for x in range(2):
                    P.pe(lambda e, h=h, x=x: e.matmul(pb[0][:, 448 + 2 * h:450 + 2 * h], lhsT=lhl[:, x, h * 128:(h + 1) * 128],
                                                      rhs=CHKb, start=(x == 0), stop=(x == 1)),
                         r=[r_cstb, r_lhl], w=[rpb[0]])
            P.act(lambda e: e.activation(out=bs, in_=pb[5][:, 0:256], func=AF.Copy), r=[rpb[5]], w=[r_bs])
            P.dve(lambda e: e.scalar_tensor_tensor(out=d1, in0=pb[5][:, 256:512], scalar=-0.5, in1=bs, op0=ALU.mult,
                                                   op1=ALU.add), r=[rpb[5], r_bs], w=[r_d1])
            P.dve(lambda e: e.tensor_tensor(out=d2, in0=pb[5][:, 256:512], in1=bs, op=ALU.subtract),
                  r=[rpb[5], r_bs], w=[r_d2])
            P.act(lambda e: e.activation(out=ex[0], in_=d1, func=AF.Exp), r=[r_d1], w=[r_ex[0]])
            P.act(lambda e: e.activation(out=ex[1], in_=d1, func=AF.Exp, scale=-1.0), r=[r_d1], w=[r_ex[1]])
            P.act(lambda e: e.activation(out=ex[2], in_=bs, func=AF.Exp), r=[r_bs], w=[r_ex[2]])
            P.act(lambda e: e.activation(out=ex[3], in_=d2, func=AF.Exp), r=[r_d2], w=[r_ex[3]])
            P.act(lambda e: e.activation(out=dec, in_=pb[0][:, 448:452], func=AF.Exp), r=[rpb[0]], w=[r_dec])
            srcs = [(qs, r_qs), (kk, r_kk), (qs, r_qs), (kk, r_kk)]
            for i in range(4):
                eng = P.dve if i % 2 == 0 else P.pool
                eng(lambda e, i=i: e.tensor_tensor(out=hb[:, i, :], in0=srcs[i][0], in1=ex[i], op=ALU.mult),
                    r=[srcs[i][1], r_ex[i]], w=[r_hb[i]])
            P.act(lambda e: e.activation(out=vb, in_=pb[3][:, 0:256], func=AF.Copy), r=[rpb[3]], w=[r_vb])
            P.act(lambda e: e.activation(out=gsil, in_=pb[3][:, 256:512], func=AF.Silu), r=[rpb[3]], w=[r_gsil])
            P.dve(lambda e: e.tensor_tensor(out=gsil, in0=gsil, in1=ogn, op=ALU.mult), r=[r_gsil, r_ogn], w=[r_gsil])
            b6 = kb.bankbf(6)
            for h in range(2):
                for j in range(3):
                    P.pe(lambda e, h=h, j=j, b6=b6: e.transpose(out=b6[:, (3 * h + j) * 128:(3 * h + j + 1) * 128],
                                                               in_=hb[:, j, h * 128:(h + 1) * 128], identity=kb.identb[:]),
                         r=[r_hb[j], kb.r_identb], w=[rpb[6]])
            P.act(lambda e, b6=b6: e.activation(out=hgT, in_=b6[:, 0:768].rearrange("p (a b) -> p a b", a=6), func=AF.Copy),
                  r=[rpb[6]], w=[r_hgT])
            for h in range(2):
                P.pe(lambda e, h=h: e.matmul(pb[7][:, h * 128:(h + 1) * 128], lhsT=hgT[:, 3 * h + 1, :], rhs=hgT[:, 3 * h + 0, :],
                                             start=True, stop=True), r=[r_hgT], w=[rpb[7]])
            P.dve(lambda e: e.tensor_tensor(out=aTm, in0=pb[7][:, 0:256].rearrange("p (a b) -> p a b", a=2),
                                            in1=HTRIb.unsqueeze(1).to_broadcast([128, 2, 128]), op=ALU.mult),
                  r=[rpb[7], r_cstb], w=[r_aTm])
            for h in range(2):
                P.pe(lambda e, h=h: e.matmul(pb[7][:, 256 + h * 128:256 + (h + 1) * 128], lhsT=aTm[:, h, :],
                                             rhs=vb[:, h * 128:(h + 1) * 128], start=(h == 0), stop=False,
                                             skip_group_check=True), r=[r_aTm, r_vb], w=[rpb[7]])
            for c in range(2):
                rows = slice(c * 64, (c + 1) * 64)
                for h in range(2):
                    P.pe(lambda e, h=h, rows=rows: e.matmul(pb[7][rows, 256 + h * 128:256 + (h + 1) * 128],
                                                            lhsT=hgT[:, 3 * h + 2, rows], rhs=Sbf[:, h, :],
                                                            start=False, stop=(c == 1), skip_group_check=True),
                         r=[r_hgT, r_Sbf[h]], w=[rpb[7]])
                for h in range(2):
                    P.pe(lambda e, h=h, rows=rows: e.matmul(pb[1][:, 256 + h * 128:256 + (h + 1) * 128],
                                                            lhsT=hb[rows, 3, h * 128:(h + 1) * 128],
                                                            rhs=vb[rows, h * 128:(h + 1) * 128], start=True, stop=True),
                         r=[r_hb[3], r_vb], w=[rpb[1]])
                for h in range(2):
                    P.dve(lambda e, h=h, c=c: e.scalar_tensor_tensor(out=S[:, h, :], in0=S[:, h, :],
                                                                     scalar=dec[:, 2 * h + c:2 * h + c + 1],
                                                                     in1=pb[1][:, 256 + h * 128:256 + (h + 1) * 128],
                                                                     op0=ALU.mult, op1=ALU.add),
                          r=[r_S[h], r_dec, rpb[1]], w=[r_S[h]])
                    P.act(lambda e, h=h: e.activation(out=Sbf[:, h, :], in_=S[:, h, :], func=AF.Copy), r=[r_S[h]], w=[r_Sbf[h]])
            for h in range(2):
                P.act(lambda e, h=h: e.activation(out=junk, in_=pb[7][:, 256 + h * 128:256 + (h + 1) * 128], func=AF.Square,
                                                  accum_out=oss[:, h:h + 1]), r=[rpb[7]], w=[r_junk, r_oss])
            P.act(lambda e: e.activation(out=oss, in_=oss, func=AF.Sqrt, scale=1.0 / 128, bias=EPS), r=[r_oss], w=[r_oss])
            P.dve(lambda e: e.reciprocal(out=oss, in_=oss), r=[r_oss], w=[r_oss])
            for h in range(2):
                P.dve(lambda e, h=h, t=t: e.scalar_tensor_tensor(out=HGO[t % 2][:, h * 128:(h + 1) * 128],
                                                                 in0=pb[7][:, 256 + h * 128:256 + (h + 1) * 128],
                                                                 scalar=oss[:, h:h + 1], in1=gsil[:, h * 128:(h + 1) * 128],
                                                                 op0=ALU.mult, op1=ALU.mult),
                      r=[rpb[7], r_oss, r_gsil], w=[r_hgo[t % 2]])
            P.dma("sp", lambda e, t=t: e.dma_start(out=io["ox"][t][:, 256:512], in_=HGO[t % 2]), r=[r_hgo[t % 2]])
        self.st = dict(QC=QC, CACHE=CACHE, r_cache=r_cache, VAUG=VAUG, r_vaug=r_vaug, GATES=GATES, r_gates=r_gates,
                       HGO=HGO, r_hgo=r_hgo, OUT=OUT, r_out=r_out, cst=cst, r_cst=r_cst, cstb=cstb, r_cstb=r_cstb,
                       W0b=W0b, Cb=Cb)
        self.pass1_end = kb.aoff


def mixer_inputs(inp, l, c, consts):
    g = c % 2
    w = inp["w_in"][l]
    cols = np.concatenate([
        np.arange(0 + 256 * g, 256 * g + 256),
        np.arange(512 + 64 * g, 512 + 64 * g + 64),
        np.arange(768 + 64 * g, 768 + 64 * g + 64),
        np.arange(1024 + 64 * g, 1024 + 64 * g + 64),
        np.arange(640 + 64 * g, 640 + 64 * g + 64),
        np.arange(896 + 64 * g, 896 + 64 * g + 64),
        np.arange(1152 + 64 * g, 1152 + 64 * g + 64),
        np.arange(1280 + 12 * g, 1280 + 12 * g + 12),
        np.arange(1304 + 256 * g, 1304 + 256 * g + 256),
        np.arange(1816 + 256 * g, 1816 + 256 * g + 256),
        np.arange(2328 + 256 * g, 2328 + 256 * g + 256),
        np.arange(2840 + 256 * g, 2840 + 256 * g + 256)])
    qn = inp["q_norm"][l]
    kn = inp["k_norm"][l]
    gqk = np.concatenate([qn, qn, qn, qn, kn[0], kn[1], kn[2]])[None, :]
    w1 = inp["cmp_w1"][l]
    w1r = np.ascontiguousarray(w1.transpose(0, 2, 1, 3).reshape(128, 32 * 128))
    w2 = inp["cmp_w2"][l]
    w2r = np.ascontiguousarray(w2.transpose(1, 0, 2).reshape(128, 128))
    pos = inp["cmp_pos"][l]
    posr = np.ascontiguousarray(pos.transpose(0, 2, 1).reshape(128, 32))
    return {
        "win": np.ascontiguousarray(w[:, cols]),
        "gqk": np.ascontiguousarray(gqk.astype(np.float32)),
        "ogn": np.ascontiguousarray(inp["hgrn_out_norm"][l][None, :]),
        "lbl": np.ascontiguousarray(inp["hgrn_lb_logits"][:, 256 * g:256 * g + 256]),
        "cw1": w1r, "cw2": w2r, "cpos": posr,
        "cst": consts["cst"], "cs": consts["cs"], "ovl": consts["ovl"],
    }


def mixer_io(nc, hin_ap=None, ox_ap=None):
    io = {
        "win": _dram_in(nc, "win", [D, NIN]),
        "gqk": _dram_in(nc, "gqk", [1, 448]),
        "ogn": _dram_in(nc, "ogn", [1, 128]),
        "lbl": _dram_in(nc, "lbl", [2, 256]),
        "cw1": _dram_in(nc, "cw1", [128, 32 * 128]),
        "cw2": _dram_in(nc, "cw2", [128, 128]),
        "cpos": _dram_in(nc, "cpos", [128, 32]),
        "cst": _dram_in(nc, "cst", [128, CST_N]),
        "cs": _dram_in(nc, "cs", [SEQ, 128]),
        "ovl": _dram_in(nc, "ovl", [256, 64]),
    }
    io["hin"] = hin_ap if hin_ap is not None else _dram_in(nc, "hin", [NQT, 128, D], BF16)
    io["ox"] = ox_ap if ox_ap is not None else _dram_out(nc, "ox", [NQT, 128, 512], BF16)
    return io


def build_phaseB(layer, pass2=True):
    nc = bass.Bass("TRN2", target_bir_lowering=False)
    kb = KB(nc)
    io = mixer_io(nc)
    mx = Mixer(kb, layer, io)
    mx.run()
    if pass2:
        mx.run2()
    kb.P.build()
    return nc, kb


def _mixer_run2(self):
    kb = self.kb
    P = kb.P
    io = self.io
    A = kb.alloc
    R = kb.res
    pb, rpb = kb.pb, kb.rpb
    st = self.st
    CACHE, r_cache, VAUG, r_vaug, GATES, r_gates = st["CACHE"], st["r_cache"], st["VAUG"], st["r_vaug"], st["GATES"], st["r_gates"]
    OUT, r_out, cst, r_cst, cstb, r_cstb, W0b, Cb = st["OUT"], st["r_out"], st["cst"], st["r_cst"], st["cstb"], st["r_cstb"], st["W0b"], st["Cb"]
    QC = st["QC"]
    ntiles = getattr(self, "ntiles", NQT)
    all_cache = r_cache[:ntiles]
    w1b = A([128, 32, 128], BF16); r_w1 = R("w1b")
    w2f = A([128, 128], F32); r_w2f = R("w2f")
    w2kd = A([128, 2, 64], BF16); w2v = A([128, 64], BF16); r_w2 = R("w2b")
    posb = A([128, 32], BF16); r_pos = R("posb")
    pbias = A([128, 2], F32); r_pbias = R("pbias")
    KCCT = A([128, 256], BF16); r_kcct = R("kcct")
    VCC = A([128, 2, 66], BF16); r_vcc = R("vcc")
    OVLb = A([128, 2, 64], BF16); r_ovl = R("ovlb")
    gx = A([128, 256], F32); r_gx = R("gx")
    gy = A([128, 256], F32); r_gy = R("gy")
    gl = [A([128, 256], BF16) for _ in range(2)]; r_gl = [R("gl0"), R("gl1")]
    P.dma("pool", lambda e: e.dma_start(out=w1b.rearrange("p a b -> p (a b)"), in_=io["cw1"], max_dma_last_dim=4096),
          w=[r_w1])
    P.dma("sp", lambda e: e.dma_start(out=w2f, in_=io["cw2"]), w=[r_w2f])
    P.dma("pool", lambda e: e.dma_start(out=posb, in_=io["cpos"]), w=[r_pos])
    P.dma("pool", lambda e: e.dma_start(out=OVLb, in_=io["ovl"].rearrange("(a p) j -> p a j", p=128)), w=[r_ovl])
    for dup in range(2):
        P.dve(lambda e, dup=dup: e.tensor_copy(out=w2kd[:, dup, :], in_=w2f[:, 0:64]), r=[r_w2f], w=[r_w2])
    P.dve(lambda e: e.tensor_copy(out=w2v, in_=w2f[:, 64:128]), r=[r_w2f], w=[r_w2])
    P.pool(lambda e: e.memset(KCCT, 0.0), w=[r_kcct])
    P.pool(lambda e: e.memset(VCC, 0.0), w=[r_vcc])
    P.pool(lambda e: e.memset(VCC[:, :, 64:66], 1.0), w=[r_vcc])
    for kv in range(2):
        rows = slice(kv * 64, (kv + 1) * 64)
        cview = CACHE[rows, 2, :].rearrange("p (n s) -> p n s", s=16)
        for l in range(32):
            P.pe(lambda e, kv=kv, rows=rows, l=l, cview=cview: e.matmul(
                pb[kv][:, 0:255], lhsT=w1b[rows, l, :], rhs=cview[:, (l // 16):(l // 16) + 255, l % 16],
                start=(l == 0), stop=(l == 31)), r=[r_w1] + all_cache, w=[rpb[kv]])
        for l in range(32):
            P.pe(lambda e, kv=kv, rows=rows, l=l: e.matmul(pb[2 + kv][:, 0:1], lhsT=w1b[rows, l, :], rhs=posb[rows, l:l + 1],
                                                           start=(l == 0), stop=(l == 31)),
                 r=[r_w1, r_pos], w=[rpb[2 + kv]])
        P.act(lambda e, kv=kv: e.activation(out=pbias[:, kv:kv + 1], in_=pb[2 + kv][:, 0:1], func=AF.Copy),
              r=[rpb[2 + kv]], w=[r_pbias])
        P.act(lambda e, kv=kv: e.activation(out=gx[:, 0:255], in_=pb[kv][:, 0:255], func=AF.Identity, bias=pbias[:, kv:kv + 1]),
              r=[rpb[kv], r_pbias], w=[r_gx])
        P.dve(lambda e: e.tensor_tensor(out=gy[:, 0:255], in0=gx[:, 0:255], in1=gx[:, 0:255], op=ALU.mult), r=[r_gx], w=[r_gy])
        P.dve(lambda e: e.tensor_scalar(out=gy[:, 0:255], in0=gy[:, 0:255], scalar1=0.044715, scalar2=1.0, op0=ALU.mult,
                                        op1=ALU.add), r=[r_gy], w=[r_gy])
        P.dve(lambda e: e.tensor_tensor(out=gy[:, 0:255], in0=gy[:, 0:255], in1=gx[:, 0:255], op=ALU.mult),
              r=[r_gy, r_gx], w=[r_gy])
        P.act(lambda e: e.activation(out=gy[:, 0:255], in_=gy[:, 0:255], func=AF.Sigmoid, scale=1.5957691216057308),
              r=[r_gy], w=[r_gy])
        P.dve(lambda e, kv=kv: e.tensor_tensor(out=gl[kv][:, 0:255], in0=gx[:, 0:255], in1=gy[:, 0:255], op=ALU.mult),
              r=[r_gx, r_gy], w=[r_gl[kv]])
    P.pe(lambda e: e.matmul(pb[2][:, 0:255], lhsT=w2kd.rearrange("p a b -> p (a b)"), rhs=gl[0][:, 0:255], start=True, stop=True),
         r=[r_w2, r_gl[0]], w=[rpb[2]])
    P.act(lambda e: e.activation(out=KCCT[:, 0:255], in_=pb[2][:, 0:255], func=AF.Copy), r=[rpb[2]], w=[r_kcct])
    for nt in range(2):
        cnt = 128 if nt == 0 else 127
        P.pe(lambda e, nt=nt, cnt=cnt: e.matmul(pb[3][0:cnt, nt * 64:(nt + 1) * 64], lhsT=gl[1][:, nt * 128:nt * 128 + cnt],
                                                rhs=w2v, start=True, stop=True), r=[r_w2, r_gl[1]], w=[rpb[3]])
        P.act(lambda e, nt=nt, cnt=cnt: e.activation(out=VCC[0:cnt, nt, 0:64], in_=pb[3][0:cnt, nt * 64:(nt + 1) * 64],
                                                     func=AF.Copy), r=[rpb[3]], w=[r_vcc])

    NE = 6
    Eb = [A([128, 4, 128], BF16) for _ in range(NE)]; r_E = [R("E%d" % i) for i in range(NE)]
    E1 = [A([128, 4, 128], BF16) for _ in range(2)]; r_E1 = [R("E1_0"), R("E1_1")]
    rsc = A([128, 12], F32); r_rsc = R("rsc")
    impf = A([128, 64], F32); r_impf = R("impf")
    impw = A([128, 64], F32); r_impw = R("impw")
    m8 = A([128, 16], F32); r_m8 = R("m8")
    selm = A([128, 64], BF16); r_selm = R("selm")
    mdiag = A([128, 128], BF16); r_mdiag = R("mdiag")
    selx = A([128, 64, 64], BF16); r_selx = R("selx")
    wgt = A([128, 12], F32); r_wgt = R("wgt")
    acc = A([128, 4, 64], F32); r_acc = R("acc")
    tmp = A([128, 4, 64], F32); r_tmp = R("tmp")
    ecnt = [0]
    sset = [0]

    def qk_exp(t, kt, ci, kbuf, r_k):
        s = sset[0] % 2
        sset[0] += 1
        bA, bB = 2 * s, 2 * s + 1
        P.pe(lambda e: e.matmul(pb[bA][:, 0:256], lhsT=kbuf[0], rhs=QC[0:64, t, :, :].rearrange("p a b -> p (a b)"),
                                start=True, stop=True), r=[r_k, r_cache[t]], w=[rpb[bA]])
        P.pe(lambda e: e.matmul(pb[bB][:, 0:256], lhsT=kbuf[1], rhs=QC[64:128, t, :, :].rearrange("p a b -> p (a b)"),
                                start=True, stop=True), r=[r_k, r_cache[t]], w=[rpb[bB]])
        return bA, bB

    for t in range(ntiles):
        ob = t % 2
        nts = [0] if t < 16 else [0, 1]
        first7 = True
        first4 = True
        for nt in nts:
            bA, bB = qk_exp(t, nt, 2, (KCCT[0:64, nt * 128:(nt + 1) * 128], KCCT[64:128, nt * 128:(nt + 1) * 128]), r_kcct)
            E = E1[nt]
            P.act(lambda e, bA=bA, E=E: e.activation(out=E[:, 0:2, :], in_=pb[bA][:, 0:256].rearrange("p (a b) -> p a b", a=2),
                                                     func=AF.Exp, scale=0.125), r=[rpb[bA]], w=[r_E1[nt]])
            P.act(lambda e, bB=bB, E=E: e.activation(out=E[:, 2:4, :], in_=pb[bB][:, 0:256].rearrange("p (a b) -> p a b", a=2),
                                                     func=AF.Exp, scale=0.125), r=[rpb[bB]], w=[r_E1[nt]])
            P.pool(lambda e, E=E, t=t, nt=nt: e.affine_select(out=E, in_=E, pattern=[[0, 4], [1, 128]], compare_op=ALU.is_ge,
                                                              fill=0.0, base=128 * t - 31 - 2048 * nt, channel_multiplier=-16),
                   r=[r_E1[nt]], w=[r_E1[nt]])
            for s in range(4):
                P.pe(lambda e, E=E, s=s, nt=nt, f=first7: e.matmul(pb[7][:, s * 66:s * 66 + 65], lhsT=E[:, s, :],
                                                                   rhs=VCC[:, nt, 0:65], start=f, stop=False,
                                                                   skip_group_check=True),
                     r=[r_E1[nt], r_vcc], w=[rpb[7]])
                first7 = False
                P.pe(lambda e, E=E, s=s, nt=nt, f=first4: e.matmul(pb[4][:, 256 + s * 64:256 + (s + 1) * 64], lhsT=E[:, s, :],
                                                                   rhs=OVLb[:, nt, :], start=f, stop=False,
                                                                   skip_group_check=True),
                     r=[r_E1[nt], r_ovl], w=[rpb[4]])
                first4 = False
        o7 = pb[7][:, 0:264].rearrange("p (s c) -> p s c", c=66)
        P.dve(lambda e, o7=o7: e.tensor_scalar(out=rsc[:, 0:4], in0=o7[:, :, 64], scalar1=1e-30, scalar2=None, op0=ALU.max),
              r=[rpb[7]], w=[r_rsc])
        P.dve(lambda e: e.reciprocal(out=rsc[:, 0:4], in_=rsc[:, 0:4]), r=[r_rsc], w=[r_rsc])
        for s in range(4):
            src = pb[4][:, 256 + s * 64:256 + (s + 1) * 64]
            if s == 0:
                P.dve(lambda e, src=src: e.tensor_scalar(out=impf, in0=src, scalar1=rsc[:, 0:1], scalar2=None, op0=ALU.mult),
                      r=[rpb[4], r_rsc], w=[r_impf])
            else:
                P.dve(lambda e, src=src, s=s: e.scalar_tensor_tensor(out=impf, in0=src, scalar=rsc[:, s:s + 1], in1=impf,
                                                                     op0=ALU.mult, op1=ALU.add),
                      r=[rpb[4], r_rsc, r_impf], w=[r_impf])
        P.dve(lambda e, t=t: e.tensor_tensor(out=impf, in0=impf, in1=cst[:, CST_BW + 64 - 2 * t:CST_BW + 128 - 2 * t], op=ALU.add),
              r=[r_impf, r_cst], w=[r_impf])
        P.dve(lambda e: e.tensor_scalar(out=impf[:, 0:1], in0=impf[:, 0:1], scalar1=1e9, scalar2=None, op0=ALU.add),
              r=[r_impf], w=[r_impf])
        P.dve(lambda e: e.max(out=m8[:, 0:8], in_=impf), r=[r_impf], w=[r_m8])
        P.dve(lambda e: e.match_replace(out=impw, in_to_replace=m8[:, 0:8], in_values=impf, imm_value=-1e30),
              r=[r_impf, r_m8], w=[r_impw])
        P.dve(lambda e: e.max(out=m8[:, 8:16], in_=impw), r=[r_impw], w=[r_m8])
        P.dve(lambda e: e.tensor_scalar(out=selm, in0=impf, scalar1=m8[:, 15:16], scalar2=None, op0=ALU.is_ge),
              r=[r_impf, r_m8], w=[r_selm])
        nj = 2 * (t + 1)
        P.pool(lambda e, nj=nj: e.tensor_copy(out=selx[:, 0:nj, :], in_=selm[:, 0:nj].unsqueeze(2).to_broadcast([128, nj, 64])),
               r=[r_selm], w=[r_selx])
        first5 = True
        for kt in range(max(0, t - 4), t + 1):
            j = kt - (t - 4)
            kc_ = slice(kt * 128, (kt + 1) * 128)
            bA, bB = qk_exp(t, kt, 4, (CACHE[0:64, 4, kc_], CACHE[64:128, 4, kc_]), r_cache[kt])
            ei = ecnt[0] % NE
            ecnt[0] += 1
            E = Eb[ei]
            P.act(lambda e, bA=bA, E=E: e.activation(out=E[:, 0:2, :], in_=pb[bA][:, 0:256].rearrange("p (a b) -> p a b", a=2),
                                                     func=AF.Exp, scale=0.125), r=[rpb[bA]], w=[r_E[ei]])
            P.act(lambda e, bB=bB, E=E: e.activation(out=E[:, 2:4, :], in_=pb[bB][:, 0:256].rearrange("p (a b) -> p a b", a=2),
                                                     func=AF.Exp, scale=0.125), r=[rpb[bB]], w=[r_E[ei]])
            if j == 0 or j == 4:
                mk = W0b if j == 0 else Cb
                P.dve(lambda e, E=E, mk=mk: e.tensor_tensor(out=E, in0=E, in1=mk.unsqueeze(1).to_broadcast([128, 4, 128]),
                                                            op=ALU.mult), r=[r_E[ei], r_cstb], w=[r_E[ei]])
            for s in range(4):
                P.pe(lambda e, E=E, s=s, kt=kt, f=first5: e.matmul(pb[5][:, s * 66:s * 66 + 65], lhsT=E[:, s, :],
                                                                   rhs=VAUG[:, kt, 1, 0:65], start=f, stop=False,
                                                                   skip_group_check=True),
                     r=[r_E[ei], r_vaug[kt]], w=[rpb[5]])
                first5 = False
        first6 = True
        for kt in range(0, t + 1):
            kc_ = slice(kt * 128, (kt + 1) * 128)
            bA, bB = qk_exp(t, kt, 3, (CACHE[0:64, 3, kc_], CACHE[64:128, 3, kc_]), r_cache[kt])
            ei = ecnt[0] % NE
            ecnt[0] += 1
            E = Eb[ei]
            P.act(lambda e, bA=bA, E=E: e.activation(out=E[:, 0:2, :], in_=pb[bA][:, 0:256].rearrange("p (a b) -> p a b", a=2),
                                                     func=AF.Exp, scale=0.125), r=[rpb[bA]], w=[r_E[ei]])
            P.act(lambda e, bB=bB, E=E: e.activation(out=E[:, 2:4, :], in_=pb[bB][:, 0:256].rearrange("p (a b) -> p a b", a=2),
                                                     func=AF.Exp, scale=0.125), r=[rpb[bB]], w=[r_E[ei]])
            ms = kt % 2
            mT = pb[4][:, ms * 128:(ms + 1) * 128]
            P.pe(lambda e, kt=kt, mT=mT: e.matmul(mT, lhsT=selx[:, 2 * kt:2 * kt + 2, :].rearrange("p a b -> p (a b)"),
                                                  rhs=kb.identb[:], start=True, stop=True, skip_group_check=True),
                 r=[r_selx, kb.r_identb], w=[rpb[4]])
            if kt == t:
                P.dve(lambda e, mT=mT: e.tensor_tensor(out=mdiag, in0=mT, in1=Cb, op=ALU.mult), r=[rpb[4], r_cstb], w=[r_mdiag])
                P.dve(lambda e, E=E: e.tensor_tensor(out=E, in0=E, in1=mdiag.unsqueeze(1).to_broadcast([128, 4, 128]),
                                                     op=ALU.mult), r=[r_E[ei], r_mdiag], w=[r_E[ei]])
            else:
                P.dve(lambda e, E=E, mT=mT: e.tensor_tensor(out=E, in0=E, in1=mT.unsqueeze(1).to_broadcast([128, 4, 128]),
                                                            op=ALU.mult), r=[r_E[ei], rpb[4]], w=[r_E[ei]])
            for s in range(4):
                P.pe(lambda e, E=E, s=s, kt=kt, f=first6: e.matmul(pb[6][:, s * 66:s * 66 + 65], lhsT=E[:, s, :],
                                                                   rhs=VAUG[:, kt, 0, 0:65], start=f, stop=False,
                                                                   skip_group_check=True),
                     r=[r_E[ei], r_vaug[kt]], w=[rpb[6]])
                first6 = False
        gv = GATES[:, t, :].rearrange("p (hh hl br) -> p br hl hh", hh=2, hl=2, br=3)
        for bi, (bank, br) in enumerate([(7, 0), (6, 1), (5, 2)]):
            ob_ = pb[bank][:, 0:264].rearrange("p (s c) -> p s c", c=66)
            P.dve(lambda e, ob_=ob_, bi=bi: e.tensor_scalar(out=rsc[:, 4 * bi:4 * bi + 4], in0=ob_[:, :, 64], scalar1=1e-30,
                                                            scalar2=None, op0=ALU.max), r=[rpb[bank]], w=[r_rsc])
            P.dve(lambda e, bi=bi: e.reciprocal(out=rsc[:, 4 * bi:4 * bi + 4], in_=rsc[:, 4 * bi:4 * bi + 4]),
                  r=[r_rsc], w=[r_rsc])
            P.dve(lambda e, bi=bi, br=br, gv=gv: e.tensor_tensor(
                out=wgt[:, 4 * bi:4 * bi + 4].rearrange("p (a b) -> p a b", a=2),
                in0=rsc[:, 4 * bi:4 * bi + 4].rearrange("p (a b) -> p a b", a=2), in1=gv[:, br], op=ALU.mult),
                r=[r_rsc, r_gates[t]], w=[r_wgt])
            dst = acc if bi == 0 else tmp
            r_dst = r_acc if bi == 0 else r_tmp
            P.dve(lambda e, ob_=ob_, bi=bi, dst=dst: e.tensor_tensor(
                out=dst, in0=ob_[:, :, 0:64], in1=wgt[:, 4 * bi:4 * bi + 4].unsqueeze(2).to_broadcast([128, 4, 64]),
                op=ALU.mult), r=[rpb[bank], r_wgt], w=[r_dst])
            if bi == 1:
                P.dve(lambda e: e.tensor_tensor(out=acc, in0=acc, in1=tmp, op=ALU.add), r=[r_acc, r_tmp], w=[r_acc])
            if bi == 2:
                ov = OUT[ob][:, 0:256].rearrange("p (hh hl d) -> p hl hh d", hh=2, hl=2)
                P.dve(lambda e, ov=ov: e.tensor_tensor(out=ov, in0=acc.rearrange("p (a b) d -> p a b d", a=2),
                                                       in1=tmp.rearrange("p (a b) d -> p a b d", a=2), op=ALU.add),
                      r=[r_acc, r_tmp], w=[r_out[ob]])
        P.dma("sp", lambda e, t=t, ob=ob: e.dma_start(out=io["ox"][t][:, 0:256], in_=OUT[ob][:, 0:256]), r=[r_out[ob]])


Mixer.run2 = _mixer_run2


def _kb_wout(self, om_tile_aps, wo, tag):
    P = self.P
    self.areset()
    wob = self.alloc([128, 8, D], BF16); r_wo = self.res("wob" + tag)
    om = [self.alloc([128, D], BF16) for _ in range(2)]; r_om = [self.res("om%d%s" % (i, tag)) for i in range(2)]
    omT = [self.alloc([128, 8, 128], BF16) for _ in range(2)]; r_omT = [self.res("omT%d%s" % (i, tag)) for i in range(2)]
    P.dma("pool", lambda e: e.dma_start(out=wob, in_=wo.rearrange("(kc p) d -> p kc d", p=128)), w=[r_wo])
    X, rX, pb, rpb = self.X, self.rX, self.pb, self.rpb
    for i in range(NT):
        b = i % 2
        for (ap, c0) in om_tile_aps(i):
            wdt = ap.shape[-1]
            P.dma("sp", lambda e, ap=ap, c0=c0, wdt=wdt, b=b: e.dma_start(out=om[b][:, c0:c0 + wdt], in_=ap), w=[r_om[b]])
        bank = 6 + b
        pbt = self.bankbf(bank)
        for kc in range(8):
            P.pe(lambda e, kc=kc, b=b, pbt=pbt: e.transpose(out=pbt[:, kc * 128:(kc + 1) * 128],
                                                           in_=om[b][:, kc * 128:(kc + 1) * 128], identity=self.identb[:]),
                 r=[r_om[b], self.r_identb], w=[rpb[bank]])
        P.act(lambda e, b=b, pbt=pbt: e.activation(out=omT[b], in_=pbt.rearrange("p (k t) -> p k t", k=8), func=AF.Copy),
              r=[rpb[bank]], w=[r_omT[b]])
        for dh in range(2):
            bk = 2 * b + dh
            for kc in range(8):
                P.pe(lambda e, bk=bk, kc=kc, b=b, dh=dh: e.matmul(pb[bk][:], lhsT=omT[b][:, kc, :],
                                                                  rhs=wob[:, kc, dh * 512:(dh + 1) * 512],
                                                                  start=(kc == 0), stop=(kc == 7)),
                     r=[r_omT[b], r_wo], w=[rpb[bk]])
            xs = X[:, i, dh * 512:(dh + 1) * 512]
            P.dve(lambda e, bk=bk, xs=xs: e.tensor_tensor(out=xs, in0=pb[bk][:], in1=xs, op=ALU.add),
                  r=[rpb[bk], rX[i]], w=[rX[i]])


KB.wout = _kb_wout
WOUT_PERM = np.concatenate([np.arange(0, 256), np.arange(512, 768), np.arange(256, 512), np.arange(768, 1024)])


def _ffn_io(nc, sfx):
    return (_dram_in(nc, "g" + sfx, [1, D]), _dram_in(nc, "wg" + sfx, [D, FF]), _dram_in(nc, "wu" + sfx, [D, FF]),
            _dram_in(nc, "wd" + sfx, [FF, D]))


def build_phaseC(with_next):
    nc = bass.Bass("TRN2", target_bir_lowering=False)
    x = _dram_in(nc, "x", [TPC, D])
    om = _dram_in(nc, "om", [NT, 128, D], BF16)
    wo = _dram_in(nc, "wo", [D, D])
    f2 = _ffn_io(nc, "2")
    if with_next:
        f1 = _ffn_io(nc, "1")
        gm = _dram_in(nc, "gm", [1, D])
        hx = _dram_out(nc, "hx", [NT, 128, D], BF16)
    y = _dram_out(nc, "y", [TPC, D])
    kb = KB(nc)
    kb.load_X(x)
    kb.wout(lambda i: [(om[i], 0)], wo, "w")
    kb.ffn(*f2, "b")
    if with_next:
        kb.ffn(*f1, "a")
    kb.store_X(y)
    if with_next:
        kb.emit_h(gm, hx, "h")
    kb.P.build()
    return nc, kb


_CONSTS = None


def _consts():
    global _CONSTS
    if _CONSTS is None:
        _CONSTS = {"cst": host_consts(), "cs": host_rope(), "ovl": host_overlap()}
    return _CONSTS


def _ffn_maps(inp, l, which, sfx):
    p = "ffn%d_" % which
    return {"g" + sfx: np.ascontiguousarray(inp[p + "norm"][l:l + 1]), "wg" + sfx: inp[p + "w_gate"][l],
            "wu" + sfx: inp[p + "w_up"][l], "wd" + sfx: inp[p + "w_down"][l]}


def kernel_multilaunch(**inputs):
    inp = {k: np.asarray(v) for k, v in inputs.items()}
    consts = _consts()
    cores = list(range(NCORE))
    xf = np.ascontiguousarray(inp["x"].reshape(-1, D))
    xs = [xf[c * TPC:(c + 1) * TPC] for c in cores]
    nc, _ = build_phaseA()
    maps = []
    for c in cores:
        m = {"x": xs[c], "g1": np.ascontiguousarray(inp["ffn1_norm"][0:1]), "wg": inp["ffn1_w_gate"][0],
             "wu": inp["ffn1_w_up"][0], "wd": inp["ffn1_w_down"][0], "gm": np.ascontiguousarray(inp["mix_norm"][0:1])}
        maps.append(m)
    res = run_bass_kernel_spmd(nc, maps, core_ids=cores)
    xs = [res.results[c]["y"] for c in cores]
    hx = [res.results[c]["hx"] for c in cores]
    for l in range(2):
        nc, _ = build_phaseB(l)
        maps = []
        for c in cores:
            b = c // 2
            m = mixer_inputs(inp, l, c, consts)
            m["hin"] = np.ascontiguousarray(np.concatenate([hx[2 * b], hx[2 * b + 1]], axis=0))
            maps.append(m)
        res = run_bass_kernel_spmd(nc, maps, core_ids=cores)
        ox = [res.results[c]["ox"] for c in cores]
        last = (l == 1)
        nc, _ = build_phaseC(not last)
        maps = []
        for c in cores:
            b, hf = c // 2, c % 2
            om = np.concatenate([ox[2 * b][16 * hf:16 * hf + 16], ox[2 * b + 1][16 * hf:16 * hf + 16]], axis=2)
            m = {"x": xs[c], "om": np.ascontiguousarray(om), "wo": np.ascontiguousarray(inp["w_out"][l][WOUT_PERM])}
            m.update(_ffn_maps(inp, l, 2, "2"))
            if not last:
                m.update(_ffn_maps(inp, l + 1, 1, "1"))
                m["gm"] = np.ascontiguousarray(inp["mix_norm"][l + 1:l + 2])
            maps.append(m)
        res = run_bass_kernel_spmd(nc, maps, core_ids=cores)
        xs = [res.results[c]["y"] for c in cores]
        if not last:
            hx = [res.results[c]["hx"] for c in cores]
    out = np.concatenate(xs, axis=0).reshape(inp["x"].shape).astype(np.float32)
    return out


def kernel(**inputs):
    return kernel_multilaunch(**inputs)
```
